# Optimizing a Trainium2 kernel written in Bass

```python
import jax, jax.numpy as jnp
from jax import lax
import numpy as np

D_MODEL = 1024
BATCH = 8
SEQ = 8192
DEPTH = 4

D_MIX = D_MODEL
N_MIXERS = 4
D_GROUP = D_MIX // N_MIXERS
HEAD_DIM = 64
N_HEADS = D_GROUP // HEAD_DIM
CONF_KERNEL = 31
SHORT_KERNEL = 3
POOL_WINDOWS = (2, 4, 8, 16)
POOL_GROUP = D_GROUP // len(POOL_WINDOWS)
CHUNK = 128
D_FF = 2816
N_IN_PIECES = 8
D_IN = N_IN_PIECES * D_GROUP
FFN_RESIDUAL = 0.5
EPS = 1e-6

kernel_name = "hybrid_macaron_parallel_conv_pool_gmlp"


def rmsnorm(x, g):
    xf = x.astype(jnp.float32)
    y = xf * lax.rsqrt(jnp.mean(xf * xf, axis=-1, keepdims=True) + EPS)
    return (y * g.astype(jnp.float32)).astype(x.dtype)


def layernorm(x, g, b):
    xf = x.astype(jnp.float32)
    mu = jnp.mean(xf, axis=-1, keepdims=True)
    xc = xf - mu
    y = xc * lax.rsqrt(jnp.mean(xc * xc, axis=-1, keepdims=True) + EPS)
    return (y * g.astype(jnp.float32) + b.astype(jnp.float32)).astype(x.dtype)


def causal_depthwise_conv(x, w):
    k, c = w.shape
    return lax.conv_general_dilated(
        x, w[:, None, :].astype(x.dtype), window_strides=(1,), padding=[(k - 1, 0)],
        dimension_numbers=("NWC", "WIO", "NWC"), feature_group_count=c)


def swiglu(h, w1, w3, w2):
    return (jax.nn.silu(h @ w1) * (h @ w3)) @ w2


def conformer_conv(val, gate, conv_w, conv_b, ln_g, ln_b):
    y = val * jax.nn.sigmoid(gate)
    y = causal_depthwise_conv(y, conv_w) + conv_b
    return jax.nn.silu(layernorm(y, ln_g, ln_b))


def short_gated_conv(b_gate, c_gate, xv, conv_w):
    return b_gate * causal_depthwise_conv(c_gate * xv, conv_w)


def multiscale_pool(xp, pool_w, pool_scale):
    bsz, s, _ = xp.shape
    xf = xp.astype(jnp.float32)
    cs = jnp.cumsum(xf, axis=1)
    pos = jnp.arange(1, s + 1, dtype=jnp.float32)[:, None]
    outs = []
    for g, w in enumerate(POOL_WINDOWS):
        sl = slice(g * POOL_GROUP, (g + 1) * POOL_GROUP)
        c = cs[..., sl]
        lagged = jnp.pad(c, ((0, 0), (w, 0), (0, 0)))[:, :s]
        mean = (c - lagged) / jnp.minimum(pos, float(w))
        outs.append(mean - xf[..., sl])
    d = jnp.stack(outs, axis=2).astype(xp.dtype)
    y = jnp.einsum("bsgc,gcd->bsgd", d, pool_w).reshape(bsz, s, D_GROUP)
    return y * pool_scale


def chunked_spatial_gating(u, v, ln_g, ln_b, w_s, b_s):
    bsz, s, _ = v.shape
    v = layernorm(v, ln_g, ln_b)
    vc = v.reshape(bsz, s // CHUNK, CHUNK, N_HEADS, HEAD_DIM)
    mask = jnp.tril(jnp.ones((CHUNK, CHUNK), dtype=bool))
    ws = jnp.where(mask[None], w_s, 0.0).astype(v.dtype)
    mixed = jnp.einsum("hts,bnshc->bnthc", ws, vc) + b_s.T[None, None, :, :, None]
    return u * mixed.reshape(bsz, s, D_GROUP)


def setup_inputs(seed: int = 0) -> dict:
    key = jax.random.key(seed)
    ks = iter(jax.random.split(key, 32))

    def nrm(shape, scale):
        return jax.random.normal(next(ks), shape, dtype=jnp.float32) * scale

    def gain(shape):
        return 1.0 + nrm(shape, 0.02)

    L = DEPTH
    return {
        "x": nrm((BATCH, SEQ, D_MODEL), 1.0),
        "ffn1_norm": gain((L, D_MODEL)),
        "ffn1_w1": nrm((L, D_MODEL, D_FF), D_MODEL ** -0.5),
        "ffn1_w3": nrm((L, D_MODEL, D_FF), D_MODEL ** -0.5),
        "ffn1_w2": nrm((L, D_FF, D_MODEL), D_FF ** -0.5),
        "mix_norm": gain((L, D_MODEL)),
        "w_in": nrm((L, D_MODEL, D_IN), D_MODEL ** -0.5),
        "conf_conv_w": nrm((L, CONF_KERNEL, D_GROUP), CONF_KERNEL ** -0.5),
        "conf_conv_b": nrm((L, D_GROUP), 0.02),
        "conf_ln_g": gain((L, D_GROUP)),
        "conf_ln_b": nrm((L, D_GROUP), 0.02),
        "sconv_w": nrm((L, SHORT_KERNEL, D_GROUP), SHORT_KERNEL ** -0.5),
        "pool_w": nrm((L, len(POOL_WINDOWS), POOL_GROUP, POOL_GROUP), POOL_GROUP ** -0.5),
        "pool_scale": 1.0 + nrm((L, D_GROUP), 0.1),
        "gmlp_ln_g": gain((L, D_GROUP)),
        "gmlp_ln_b": nrm((L, D_GROUP), 0.02),
        "gmlp_w_s": nrm((L, N_HEADS, CHUNK, CHUNK), CHUNK ** -0.5),
        "gmlp_b_s": 1.0 + nrm((L, N_HEADS, CHUNK), 0.02),
        "w_out": nrm((L, D_MIX, D_MODEL), D_MIX ** -0.5),
        "ffn2_norm": gain((L, D_MODEL)),
        "ffn2_w1": nrm((L, D_MODEL, D_FF), D_MODEL ** -0.5),
        "ffn2_w3": nrm((L, D_MODEL, D_FF), D_MODEL ** -0.5),
        "ffn2_w2": nrm((L, D_FF, D_MODEL), D_FF ** -0.5),
        "final_norm": gain((D_MODEL,)),
    }


def reference(x, ffn1_norm, ffn1_w1, ffn1_w3, ffn1_w2, mix_norm, w_in,
              conf_conv_w, conf_conv_b, conf_ln_g, conf_ln_b, sconv_w,
              pool_w, pool_scale, gmlp_ln_g, gmlp_ln_b, gmlp_w_s, gmlp_b_s,
              w_out, ffn2_norm, ffn2_w1, ffn2_w3, ffn2_w2, final_norm):
    for l in range(DEPTH):
        h = rmsnorm(x, ffn1_norm[l])
        x = x + FFN_RESIDUAL * swiglu(h, ffn1_w1[l], ffn1_w3[l], ffn1_w2[l])

        h = rmsnorm(x, mix_norm[l])
        p = h @ w_in[l]
        a_val, a_gate, s_b, s_c, s_x, pool_in, g_u, g_v = jnp.split(p, N_IN_PIECES, axis=-1)

        y_a = conformer_conv(a_val, a_gate, conf_conv_w[l], conf_conv_b[l],
                             conf_ln_g[l], conf_ln_b[l])
        y_b = short_gated_conv(s_b, s_c, s_x, sconv_w[l])
        y_c = multiscale_pool(pool_in, pool_w[l], pool_scale[l])
        y_d = chunked_spatial_gating(g_u, g_v, gmlp_ln_g[l], gmlp_ln_b[l],
                                     gmlp_w_s[l], gmlp_b_s[l])

        mix = jnp.concatenate([y_a, y_b, y_c, y_d], axis=-1)
        x = x + mix @ w_out[l]

        h = rmsnorm(x, ffn2_norm[l])
        x = x + FFN_RESIDUAL * swiglu(h, ffn2_w1[l], ffn2_w3[l], ffn2_w2[l])

    return rmsnorm(x, final_norm)
```

```python
import contextlib
import numpy as np
import concourse.bass as bass
import concourse.mybir as mybir
from concourse.bass_utils import run_bass_kernel_spmd

F32 = mybir.dt.float32
BF16 = mybir.dt.bfloat16
ALU = mybir.AluOpType
AF = mybir.ActivationFunctionType

PE, ACT, DVE, POOL, SP = "tensor", "scalar", "vector", "gpsimd", "sync"
ENGS = (PE, ACT, DVE, POOL, SP)
BLK = 512

D = 1024
DFF = 2816
NJ = DFF // 128
DIN = 2048
DEPTH = 4
EPS = 1e-6
T = 1024
SBW = 512
NSB = T // SBW
NSLOT = 4
SLOT_BYTES = 16384
CK = 31
DBG_STAGES = ("ffn1", "mixer", "ffn2")


class V:
    __slots__ = ("ap", "keys")

    def __init__(self, ap, keys):
        self.ap = ap
        self.keys = keys


class Buf:
    def __init__(self, nc, arena_addr, name, off, shape, dtype):
        self.esz = 2 if dtype == BF16 else 4
        self.shape = tuple(shape)
        self.off = off
        n = 1
        for s in shape:
            n *= s
        self.nbytes = n * self.esz
        self.t = nc.alloc_sbuf_tensor_at(name, [128] + list(shape), dtype, offset=arena_addr + off)
        strides = []
        st = 1
        for s in reversed(shape):
            strides.append(st)
            st *= s
        self.strides = tuple(reversed(strides))
        self._cache = {}

    def v(self, *idx, p=None):
        ck = (idx, p)
        try:
            hit = self._cache.get(ck)
        except TypeError:
            hit = None
            ck = None
        if hit is not None:
            return hit
        full = list(idx) + [slice(None)] * (len(self.shape) - len(idx))
        los, his = [], []
        for ix, n in zip(full, self.shape):
            if isinstance(ix, int):
                los.append(ix); his.append(ix + 1)
            else:
                a = 0 if ix.start is None else ix.start
                b = n if ix.stop is None else ix.stop
                assert 0 <= a < b <= n, (self.shape, idx)
                los.append(a); his.append(b)
        nd = len(self.shape)
        k = nd - 1
        while k > 0 and los[k] == 0 and his[k] == self.shape[k]:
            k -= 1
        keys = set()
        run_lo = los[k] * self.strides[k]
        run_hi = his[k] * self.strides[k]

        def rec(d, base):
            if d == k:
                lo = self.off + (base + run_lo) * self.esz
                hi = self.off + (base + run_hi) * self.esz
                keys.update(range(lo // BLK, (hi - 1) // BLK + 1))
                return
            for i in range(los[d], his[d]):
                rec(d + 1, base + i * self.strides[d])
        rec(0, 0)
        pidx = slice(None) if p is None else slice(p[0], p[1])
        ap = self.t[tuple([pidx] + full)]
        res = V(ap, tuple(keys))
        if ck is not None:
            self._cache[ck] = res
        return res


def sl(a, b):
    return slice(a, b)


class Ins:
    __slots__ = ("eng", "fn", "idx", "deps", "signal", "sigval", "is_dma", "sem", "waits")


class Sched:
    def __init__(self):
        self.streams = {e: [] for e in ENGS}
        self.last_w = {}
        self.readers = {}
        self.dma_cnt = {}

    def add(self, eng, fns, reads=(), writes=(), dma_sem=None):
        if not isinstance(fns, (list, tuple)):
            fns = [fns]
        raw = set()
        oth = set()
        last_w = self.last_w
        readers = self.readers
        for k in reads:
            w = last_w.get(k)
            if w is not None:
                raw.add(w)
        for k in writes:
            w = last_w.get(k)
            if w is not None:
                oth.add(w)
            rs = readers.get(k)
            if rs:
                oth.update(rs.values())
        final = []
        for group, is_raw in ((raw, True), (oth, False)):
            best = {}
            for d in group:
                if d.is_dma:
                    final.append((d, is_raw))
                else:
                    b = best.get(d.eng)
                    if b is None or d.idx > b.idx:
                        best[d.eng] = d
            final.extend((d, is_raw) for d in best.values())
        last = None
        st = self.streams[eng]
        is_dma = dma_sem is not None
        for i, fn in enumerate(fns):
            ins = Ins()
            ins.eng = eng; ins.fn = fn; ins.idx = len(st)
            ins.deps = final if i == 0 else ()
            ins.signal = False; ins.sigval = 0
            ins.is_dma = is_dma; ins.sem = dma_sem
            if is_dma:
                c = self.dma_cnt.get(dma_sem, 0) + 1
                self.dma_cnt[dma_sem] = c
                ins.sigval = 16 * c
            st.append(ins)
            last = ins
        rkey = (eng, dma_sem, last.idx) if is_dma else eng
        for k in reads:
            r = readers.get(k)
            if r is None:
                readers[k] = {rkey: last}
            else:
                r[rkey] = last
        for k in writes:
            last_w[k] = last
            readers[k] = None
        return last

    def finalize(self):
        for e in ENGS:
            for ins in self.streams[e]:
                waits = []
                for (d, is_raw) in ins.deps:
                    if d.is_dma:
                        waits.append(d)
                    elif d.eng == ins.eng:
                        if ins.eng == PE and not ins.is_dma:
                            continue
                        waits.append(d); d.signal = True
                    else:
                        waits.append(d); d.signal = True
                ins.waits = waits
                ins.deps = None
        for e in ENGS:
            c = 0
            for ins in self.streams[e]:
                if ins.is_dma:
                    continue
                if ins.signal:
                    c += 1
                    ins.sigval = c

    def emit(self, block, sems, dma_sems):
        self.finalize()
        streams = self.streams

        def run(eng_name):
            def body(e):
                waited = {}
                for ins in streams[eng_name]:
                    for d in ins.waits:
                        s = dma_sems[d.sem] if d.is_dma else sems[d.eng]
                        key = id(s)
                        if waited.get(key, 0) >= d.sigval:
                            continue
                        waited[key] = d.sigval
                        e.wait_ge(s, d.sigval)
                    r = ins.fn(e)
                    if ins.is_dma:
                        r.then_inc(dma_sems[ins.sem], 16)
                    elif ins.signal:
                        r.then_inc(sems[eng_name], 1)
            return body
        block.tensor(run(PE))
        block.scalar(run(ACT))
        block.vector(run(DVE))
        block.gpsimd(run(POOL))
        block.sync(run(SP))


def _cst_layout(nL):
    off = {}
    c = 0

    def put(name, n):
        nonlocal c
        off[name] = c
        c += n
    put("n1", nL * 8); put("nm", nL * 8); put("n2", nL * 8); put("nf", 8)
    put("cw", nL * 2 * CK); put("cb", nL * 2); put("clg", nL * 2); put("clb", nL * 2)
    put("sw", nL * 2 * 3); put("psc", nL * 2); put("invw", 2); put("invc", 2 * 16)
    put("glg", nL * 256); put("glb", nL * 256); put("bias", nL * 2 * 128); put("ident", 128)
    off["_keep"] = c
    put("wst", nL * 4 * 128); put("mask", 128); put("poolw", nL * 2 * 128)
    off["_total"] = c
    return off


def _pack_consts(inp, layers, with_final):
    nL = len(layers)
    lo = _cst_layout(nL)
    cst = np.zeros((128, lo["_total"]), np.float32)

    def fm(vec):
        return np.asarray(vec, np.float32).reshape(8, 128).T

    def cm(vec):
        return np.asarray(vec, np.float32).reshape(2, 128).T
    for li, l in enumerate(layers):
        cst[:, lo["n1"] + li * 8: lo["n1"] + li * 8 + 8] = fm(inp["ffn1_norm"][l])
        cst[:, lo["nm"] + li * 8: lo["nm"] + li * 8 + 8] = fm(inp["mix_norm"][l])
        cst[:, lo["n2"] + li * 8: lo["n2"] + li * 8 + 8] = fm(inp["ffn2_norm"][l])
        cw = np.asarray(inp["conf_conv_w"][l], np.float32)
        sw = np.asarray(inp["sconv_w"][l], np.float32)
        for i in range(2):
            cst[:, lo["cw"] + (li * 2 + i) * CK: lo["cw"] + (li * 2 + i + 1) * CK] = cw[:, i * 128:(i + 1) * 128].T
            cst[:, lo["sw"] + (li * 2 + i) * 3: lo["sw"] + (li * 2 + i + 1) * 3] = sw[:, i * 128:(i + 1) * 128].T
        cst[:, lo["cb"] + li * 2: lo["cb"] + li * 2 + 2] = cm(inp["conf_conv_b"][l])
        cst[:, lo["clg"] + li * 2: lo["clg"] + li * 2 + 2] = cm(inp["conf_ln_g"][l])
        cst[:, lo["clb"] + li * 2: lo["clb"] + li * 2 + 2] = cm(inp["conf_ln_b"][l])
        cst[:, lo["psc"] + li * 2: lo["psc"] + li * 2 + 2] = cm(inp["pool_scale"][l])
        cst[:, lo["glg"] + li * 256: lo["glg"] + (li + 1) * 256] = np.asarray(inp["gmlp_ln_g"][l], np.float32)[None, :]
        cst[:, lo["glb"] + li * 256: lo["glb"] + (li + 1) * 256] = np.asarray(inp["gmlp_ln_b"][l], np.float32)[None, :]
        bs = np.asarray(inp["gmlp_b_s"][l], np.float32)
        ws = np.asarray(inp["gmlp_w_s"][l], np.float32)
        pw = np.asarray(inp["pool_w"][l], np.float32)
        for i in range(2):
            b0 = lo["bias"] + (li * 2 + i) * 128
            cst[0:64, b0:b0 + 128] = bs[2 * i][None, :]
            cst[64:128, b0:b0 + 128] = bs[2 * i + 1][None, :]
            p0 = lo["poolw"] + (li * 2 + i) * 128
            cst[0:64, p0:p0 + 64] = pw[2 * i]
            cst[64:128, p0 + 64:p0 + 128] = pw[2 * i + 1]
        for h in range(4):
            w0 = lo["wst"] + (li * 4 + h) * 128
            cst[:, w0:w0 + 128] = ws[h].T
    if with_final:
        cst[:, lo["nf"]: lo["nf"] + 8] = fm(inp["final_norm"])
    wins = np.array([[2, 4], [8, 16]], np.float32)
    for i in range(2):
        for hf in range(2):
            ps = slice(hf * 64, hf * 64 + 64)
            cst[ps, lo["invw"] + i] = 1.0 / wins[i, hf]
            for t in range(16):
                cst[ps, lo["invc"] + i * 16 + t] = 1.0 / min(t + 1, wins[i, hf])
    cst[:, lo["ident"]: lo["ident"] + 128] = np.eye(128, dtype=np.float32)
    s_idx = np.arange(128)[:, None]
    t_idx = np.arange(128)[None, :]
    cst[:, lo["mask"]: lo["mask"] + 128] = (s_idx <= t_idx).astype(np.float32)
    return cst, lo


def build_program(S_len, nL, with_final):
    assert S_len % T == 0
    NT = S_len // T
    nc = bass.Bass("TRN2", target_bir_lowering=False)
    lo = _cst_layout(nL)
    NKEEP = lo["_keep"]
    NTMP = lo["_total"] - NKEEP

    def dram(name, shape, kind="ExternalInput"):
        return nc.dram_tensor(name, shape, F32, kind=kind).ap()
    xT = dram("xT", [D, S_len])
    yT = dram("yT", [D, S_len], "ExternalOutput")
    cst_d = dram("cst", [128, lo["_total"]])
    wd = {}
    for f in ("ffn1", "ffn2"):
        wd[f + "_w1"] = dram(f + "_w1", [nL, D, DFF])
        wd[f + "_w3"] = dram(f + "_w3", [nL, D, DFF])
        wd[f + "_w2"] = dram(f + "_w2", [nL, DFF, D])
    wd["w_in"] = dram("w_in", [nL, D, DIN])
    wd["w_out"] = dram("w_out", [nL, D, D])

    arena = nc.alloc_sbuf_tensor("arena", [128, nc.sbuf_bytes_remaining - 64], mybir.dt.uint8)
    base = nc.lookup_mloc(arena).addr
    arena_size = nc.lookup_mloc(arena).dims[1]
    cur = [0]

    def mk(name, shape, dt, at=None):
        o = cur[0] if at is None else at
        b = Buf(nc, base, name, o, shape, dt)
        if at is None:
            cur[0] = o + (b.nbytes + BLK - 1) // BLK * BLK
            assert cur[0] <= arena_size, (name, cur[0], arena_size)
        return b

    X = mk("X", [8, T], F32)
    H = mk("H", [8, T], BF16)
    RSTD = mk("RSTD", [2, SBW], F32)
    CK_ = mk("CKEEP", [NKEEP], F32)
    WST = mk("WST", [nL, 4, 128], BF16)
    POOLW = mk("POOLW", [nL, 2, 128], BF16)
    ONESD = mk("ONESD", [128], BF16)
    ONES256 = mk("ONES256", [128], BF16)
    HY = mk("HY", [nL, 2, 32], F32)
    HU = mk("HU", [nL, 2, 2], F32)
    HP = mk("HP", [nL, 2, 16], F32)
    VNP = mk("VNP", [4, 2, 2, 128], BF16)
    DD = mk("DD", [2, SBW], BF16)
    MEZ = mk("MEZ", [2, SBW], F32)
    ring_off = cur[0]
    cur[0] += NSLOT * SLOT_BYTES
    R13 = [mk(f"R13_{s}", [2, 8, 512], BF16, at=ring_off + s * SLOT_BYTES) for s in range(NSLOT)]
    R2 = [mk(f"R2_{s}", [NJ, 256], BF16, at=ring_off + s * SLOT_BYTES) for s in range(NSLOT)]
    RIN = [mk(f"RIN_{s}", [8, 1024], BF16, at=ring_off + s * SLOT_BYTES) for s in range(NSLOT)]
    RDG = [mk(f"RDG_{s}", [2, CK, 128], BF16, at=ring_off + s * SLOT_BYTES) for s in range(NSLOT)]
    u0 = cur[0]
    G = mk("G", [NJ, T], BF16)
    SQ = mk("SQ", [8, SBW], BF16)
    STMP = mk("STMP", [2, SBW], F32)
    ffn_end = cur[0]
    cur[0] = u0
    MIX = mk("MIX", [8, T], BF16)
    YB = mk("YB", [2, 768], BF16)
    UB = mk("UB", [2, 640], F32)
    PB = mk("PB", [2, 640], F32)
    GU = mk("GU", [2, SBW], F32)
    SBB = mk("SBB", [2, SBW], F32)
    s0 = cur[0]
    KB = 1024
    SG = mk("SG", [2, SBW], F32, at=s0)
    Z = mk("Z", [2, SBW], F32, at=s0 + 10 * KB)
    ZSQ = mk("ZSQ", [2, SBW], BF16, at=s0 + 14 * KB)
    ZB = mk("ZB", [2, SBW], BF16, at=s0 + 16 * KB)
    LNA = mk("LNA", [3, SBW], F32, at=s0)
    SC = mk("SC", [2, SBW], F32, at=s0)
    ACC = mk("ACC", [2, SBW], F32, at=s0 + 4 * KB)
    PS2 = mk("PS2", [2, 640], F32, at=s0)
    PS4 = mk("PS4", [640], F32, at=s0 + 5 * KB)
    PS8 = mk("PS8", [640], F32, at=s0 + 5 * KB + 2560)
    SSEL = mk("SSEL", [2, SBW], F32, at=s0 + 10 * KB)
    PTMP = mk("PTMP", [2, 16], F32, at=s0 + 14 * KB)
    VT = mk("VT", [4, 256], F32, at=s0 + 10 * KB)
    T1 = mk("T1", [2, SBW], F32, at=s0)
    BNS = mk("BNS", [4, 6], F32, at=s0 + 8 * KB)
    MV = mk("MV", [4, 2], F32, at=s0 + 8 * KB + 512)
    RS = mk("RS", [4], F32, at=s0 + 8 * KB + 1024)
    cur[0] = max(s0 + 18 * KB, ffn_end)
    assert cur[0] <= arena_size, (cur[0], arena_size)
    CTMP = mk("CTMP", [NTMP], F32, at=H.off)
    assert CTMP.nbytes <= H.nbytes

    ps_t = [nc.alloc_psum_tensor(f"ps{i}", [128, SBW], F32) for i in range(8)]
    ps_k = [0]

    def newps():
        ps_k[0] = (ps_k[0] + 1) % 8
        return ps_k[0]

    def P(i, a=0, b=SBW):
        return V(ps_t[i][:, a:b], (("P", i),))

    S = Sched()

    def cs(name, j=0, n=1):
        o = lo[name] + j
        return CK_.v(sl(o, o + n))

    cview = CK_.v()
    tview = CTMP.v()
    S.add(SP, [lambda e: e.dma_start(out=cview.ap, in_=cst_d[:, 0:NKEEP]),
               lambda e: e.dma_start(out=tview.ap, in_=cst_d[:, NKEEP:NKEEP + NTMP])],
          writes=cview.keys + tview.keys, dma_sem="c")
    S.add(DVE, lambda e: e.memset(ONESD.v().ap, 1.0 / D), writes=ONESD.v().keys)
    S.add(DVE, lambda e: e.memset(ONES256.v().ap, 1.0 / 256), writes=ONES256.v().keys)
    S.add(DVE, lambda e: e.memset(VNP.v().ap, 0.0), writes=VNP.v().keys)
    S.add(DVE, lambda e: e.memset(HY.v().ap, 0.0), writes=HY.v().keys)
    S.add(DVE, lambda e: e.memset(HU.v().ap, 0.0), writes=HU.v().keys)
    S.add(DVE, lambda e: e.memset(HP.v().ap, 0.0), writes=HP.v().keys)
    o_w = lo["wst"] - NKEEP
    o_m = lo["mask"] - NKEEP
    o_p = lo["poolw"] - NKEEP
    wst_in = CTMP.t[:, o_w:o_w + nL * 4 * 128].rearrange("p (a t) -> p a t", t=128)
    mask_bc = CTMP.t[:, o_m:o_m + 128].unsqueeze(1).to_broadcast([128, nL * 4, 128])
    wst_out = WST.t[:].rearrange("p l h t -> p (l h) t")
    S.add(DVE, lambda e: e.tensor_tensor(out=wst_out, in0=wst_in, in1=mask_bc, op=ALU.mult),
          reads=tview.keys, writes=WST.v().keys)
    pw_in = CTMP.t[:, o_p:o_p + nL * 2 * 128]
    pw_out = POOLW.t[:].rearrange("p l i d -> p (l i d)")
    S.add(DVE, lambda e: e.tensor_copy(out=pw_out, in_=pw_in), reads=tview.keys, writes=POOLW.v().keys)

    ring_k = [0]

    pinned = set()

    def next_slot():
        while True:
            s = ring_k[0] % NSLOT
            ring_k[0] += 1
            if s not in pinned:
                return s

    def wview(name, li, kc):
        return wd[name][li].rearrange("(c p) n -> p c n", p=128)

    def mm_group(outv, pairs):
        n = len(pairs)
        for i, (l, r) in enumerate(pairs):
            S.add(PE, lambda e, l=l, r=r, i=i: e.matmul(outv.ap, l.ap, r.ap, start=(i == 0), stop=(i == n - 1)),
                  reads=l.keys + r.keys, writes=outv.keys)

    class Norm:
        def __init__(self, gname, li):
            self.gname = gname
            self.li = li
            self.pb = {}
            self.done = set()

        def A(self, sb):
            ts = sl(sb * SBW, (sb + 1) * SBW)
            xin = X.v(sl(0, 8), ts)
            sq = SQ.v()
            S.add(ACT, lambda e, xin=xin, sq=sq: e.activation(out=sq.ap, in_=xin.ap, func=AF.Square),
                  reads=xin.keys, writes=sq.keys)

        def B(self, sb):
            pb = newps()
            mm_group(P(pb), [(ONESD.v(), SQ.v(c)) for c in range(8)])
            rs = RSTD.v(sb)
            S.add(ACT, lambda e, rs=rs, pb=pb: e.activation(out=rs.ap, in_=P(pb).ap, func=AF.Ln, bias=EPS, scale=1.0),
                  reads=P(pb).keys, writes=rs.keys)
            S.add(ACT, lambda e, rs=rs: e.activation(out=rs.ap, in_=rs.ap, func=AF.Exp, scale=-0.5), reads=rs.keys, writes=rs.keys)

        def C(self, sb):
            ts = sl(sb * SBW, (sb + 1) * SBW)
            rs = RSTD.v(sb)
            for c in range(8):
                xv = X.v(c, ts)
                hv = H.v(c, ts)
                gv = cs(self.gname, self.li * 8 + c)
                S.add(DVE, lambda e, xv=xv, hv=hv, gv=gv, rs=rs: e.scalar_tensor_tensor(
                    out=hv.ap, in0=xv.ap, scalar=gv.ap, in1=rs.ap, op0=ALU.mult, op1=ALU.mult),
                    reads=xv.keys + gv.keys + rs.keys, writes=hv.keys)
            self.done.add(sb)

        def plain(self):
            for sb in range(NSB):
                self.A(sb); self.B(sb); self.C(sb)

    def run_tail(items0, items1, nxt):
        for f in items0:
            f()
        if nxt is None:
            for f in items1:
                f()
            return
        nxt.A(0)
        h = max(1, (3 * len(items1)) // 4)
        for f in items1[:h]:
            f()
        nxt.B(0)
        nxt.C(0)
        for f in items1[h:]:
            f()
        nxt.A(1)

    diag_of = {}

    def build_diag(li):
        sD = next_slot()
        pinned.add(sD)
        DG = RDG[sD]
        o_id = lo["ident"]
        ident_bc = CK_.t[:, o_id:o_id + 128].unsqueeze(1).to_broadcast([128, CK, 128])
        for i in range(2):
            o_cw = lo["cw"] + (li * 2 + i) * CK
            cw_bc = CK_.t[:, o_cw:o_cw + CK].unsqueeze(2).to_broadcast([128, CK, 128])
            dgv = DG.v(i)
            S.add(DVE, lambda e, dgv=dgv, cw_bc=cw_bc: e.tensor_tensor(out=dgv.ap, in0=ident_bc, in1=cw_bc, op=ALU.mult),
                  reads=cs("ident", 0, 128).keys + cs("cw", (li * 2 + i) * CK, CK).keys, writes=dgv.keys)
        diag_of[li] = (DG, sD)

    def ffn(fname, norm, li, mid_hook=None, nxt=None):
        if not norm.done and not norm.pb.get("pre"):
            norm.plain()
        w1v = wview(fname + "_w1", li, 8)
        w3v = wview(fname + "_w3", li, 8)
        w2v = wd[fname + "_w2"][li].rearrange("(j p) n -> p j n", p=128)
        j0 = 0
        stk = 0
        while j0 < NJ:
            nj = min(4, NJ - j0)
            s = next_slot()
            ncol = nj * 128
            d1 = R13[s].v(0, sl(0, 8), sl(0, ncol))
            d3 = R13[s].v(1, sl(0, 8), sl(0, ncol))
            c0 = j0 * 128
            S.add(POOL, [lambda e, d1=d1, c0=c0, ncol=ncol: e.dma_start(out=d1.ap, in_=w1v[:, :, c0:c0 + ncol]),
                         lambda e, d3=d3, c0=c0, ncol=ncol: e.dma_start(out=d3.ap, in_=w3v[:, :, c0:c0 + ncol])],
                  writes=d1.keys + d3.keys, dma_sem=f"w{s}")
            order = [(jj, sb) for jj in range(nj) for sb in range(NSB)]
            if j0 == 0:
                order = [(jj, sb) for sb in range(NSB) for jj in range(nj)]
            for oi, (jj, sb) in enumerate(order):
                    if j0 == 0 and oi == 1 and norm.pb.get("pre"):
                        norm.B(1); norm.C(1)
                    j = j0 + jj
                    ts = sl(sb * SBW, (sb + 1) * SBW)
                    pa = newps()
                    mm_group(P(pa), [(R13[s].v(0, c, sl(jj * 128, jj * 128 + 128)), H.v(c, ts)) for c in range(8)])
                    pb = newps()
                    mm_group(P(pb), [(R13[s].v(1, c, sl(jj * 128, jj * 128 + 128)), H.v(c, ts)) for c in range(8)])
                    st = STMP.v(stk % 2)
                    stk += 1
                    S.add(ACT, lambda e, st=st, pa=pa: e.activation(out=st.ap, in_=P(pa).ap, func=AF.Silu),
                          reads=P(pa).keys, writes=st.keys)
                    gv = G.v(j, ts)
                    S.add(DVE, lambda e, st=st, pb=pb, gv=gv: e.tensor_tensor(out=gv.ap, in0=P(pb).ap, in1=st.ap, op=ALU.mult),
                          reads=P(pb).keys + st.keys, writes=gv.keys)
            j0 += nj
        if mid_hook is not None:
            mid_hook()
        for g2 in range(4):
            s = next_slot()
            dv = R2[s].v()
            c0 = g2 * 256
            S.add(POOL, lambda e, dv=dv, c0=c0: e.dma_start(out=dv.ap, in_=w2v[:, :, c0:c0 + 256]),
                  writes=dv.keys, dma_sem=f"w{s}")
            def item(cc, sb, s=s, g2=g2):
                c = g2 * 2 + cc
                ts = sl(sb * SBW, (sb + 1) * SBW)
                po = newps()
                mm_group(P(po), [(R2[s].v(j, sl(cc * 128, cc * 128 + 128)), G.v(j, ts)) for j in range(NJ)])
                xv = X.v(c, ts)
                S.add(DVE, lambda e, xv=xv, po=po: e.scalar_tensor_tensor(
                    out=xv.ap, in0=P(po).ap, scalar=0.5, in1=xv.ap, op0=ALU.mult, op1=ALU.add),
                    reads=P(po).keys + xv.keys, writes=xv.keys)
            if g2 < 3:
                for cc in range(2):
                    for sb in range(NSB):
                        item(cc, sb)
            else:
                run_tail([lambda cc=cc: item(cc, 0) for cc in range(2)],
                         [lambda cc=cc: item(cc, 1) for cc in range(2)], nxt)
                if nxt is not None:
                    nxt.pb["pre"] = True

    def mixer(li, first_tile, norm, nxt=None):
        if not norm.done and not norm.pb.get("pre"):
            norm.plain()
        win = wview("w_in", li, 8)
        sA = next_slot()
        dA = RIN[sA].v()
        S.add(POOL, lambda e: e.dma_start(out=dA.ap, in_=win[:, :, 0:1024]), writes=dA.keys, dma_sem=f"w{sA}")
        sB = next_slot()
        dB = RIN[sB].v()
        S.add(POOL, lambda e: e.dma_start(out=dB.ap, in_=win[:, :, 1024:2048]), writes=dB.keys, dma_sem=f"w{sB}")
        sO = next_slot()
        dO = RIN[sO].v()
        wout = wview("w_out", li, 8)
        S.add(POOL, lambda e: e.dma_start(out=dO.ap, in_=wout), writes=dO.keys, dma_sem=f"w{sO}")

        def proj(m, ts):
            slot = sA if m < 8 else sB
            mc = (m % 8) * 128
            pb = newps()
            mm_group(P(pb), [(RIN[slot].v(c, sl(mc, mc + 128)), H.v(c, ts)) for c in range(8)])
            return pb

        DG, sD_ = diag_of[li]

        for sb in range(NSB):
            ts = sl(sb * SBW, (sb + 1) * SBW)
            first_sb = first_tile and sb == 0
            for (HB, BB, n) in ((HY, YB, 30), (HU, UB, 2), (HP, PB, 16)):
                src = HB.v(li, sl(0, 2), sl(0, n))
                dst = BB.v(sl(0, 2), sl(0, n))
                S.add(ACT, lambda e, src=src, dst=dst: e.copy(out=dst.ap, in_=src.ap), reads=src.keys, writes=dst.keys)
            for i in range(2):
                pg = proj(2 + i, ts)
                sg = SG.v(i)
                S.add(ACT, lambda e, sg=sg, pg=pg: e.activation(out=sg.ap, in_=P(pg).ap, func=AF.Sigmoid),
                      reads=P(pg).keys, writes=sg.keys)
                pv = proj(i, ts)
                yv = YB.v(i, sl(30, 30 + SBW))
                S.add(DVE, lambda e, yv=yv, pv=pv, sg=sg: e.tensor_tensor(out=yv.ap, in0=P(pv).ap, in1=sg.ap, op=ALU.mult),
                      reads=P(pv).keys + sg.keys, writes=yv.keys)
            srcv = YB.v(sl(0, 2), sl(SBW, SBW + 30))
            dstv = HY.v(li, sl(0, 2), sl(0, 30))
            S.add(ACT, lambda e, srcv=srcv, dstv=dstv: e.copy(out=dstv.ap, in_=srcv.ap), reads=srcv.keys, writes=dstv.keys)
            if sb == 0 and norm.pb.get("pre"):
                norm.B(1); norm.C(1)

            for i in range(2):
                pc = proj(6 + i, ts)
                scv = SC.v(i)
                S.add(ACT, lambda e, scv=scv, pc=pc: e.copy(out=scv.ap, in_=P(pc).ap), reads=P(pc).keys, writes=scv.keys)
                pbb = proj(4 + i, ts)
                sbv = SBB.v(i)
                S.add(ACT, lambda e, sbv=sbv, pbb=pbb: e.copy(out=sbv.ap, in_=P(pbb).ap), reads=P(pbb).keys, writes=sbv.keys)
            for i in range(2):
                scv = SC.v(i)
                px = proj(8 + i, ts)
                uv = UB.v(i, sl(2, 2 + SBW))
                S.add(DVE, lambda e, uv=uv, px=px, scv=scv: e.tensor_tensor(out=uv.ap, in0=P(px).ap, in1=scv.ap, op=ALU.mult),
                      reads=P(px).keys + scv.keys, writes=uv.keys)
            for i in range(2):
                pp = proj(10 + i, ts)
                pv = PB.v(i, sl(16, 16 + SBW))
                S.add(ACT, lambda e, pv=pv, pp=pp: e.copy(out=pv.ap, in_=P(pp).ap), reads=P(pp).keys, writes=pv.keys)
            for i in range(2):
                pu = proj(12 + i, ts)
                guv = GU.v(i)
                S.add(ACT, lambda e, guv=guv, pu=pu: e.copy(out=guv.ap, in_=P(pu).ap), reads=P(pu).keys, writes=guv.keys)
            pvb = []
            for half in range(2):
                pb = newps()
                for q in range(2):
                    blk = half * 2 + q
                    t0 = sb * SBW + blk * 128
                    mm_group(P(pb, q * 256, q * 256 + 256),
                             [(H.v(c, sl(t0, t0 + 128)), RIN[sB].v(c, sl(768, 1024))) for c in range(8)])
                pvb.append(pb)

            for k in range(3):
                for i in range(2):
                    uin = UB.v(i, sl(k, k + SBW))
                    av = ACC.v(i)
                    wv = cs("sw", (li * 2 + i) * 3 + k)
                    if k == 0:
                        S.add(DVE, lambda e, uin=uin, av=av, wv=wv: e.tensor_scalar(
                            out=av.ap, in0=uin.ap, scalar1=wv.ap, scalar2=None, op0=ALU.mult),
                            reads=uin.keys + wv.keys, writes=av.keys)
                    else:
                        S.add(DVE, lambda e, uin=uin, av=av, wv=wv: e.scalar_tensor_tensor(
                            out=av.ap, in0=uin.ap, scalar=wv.ap, in1=av.ap, op0=ALU.mult, op1=ALU.add),
                            reads=uin.keys + wv.keys + av.keys, writes=av.keys)
            for i in range(2):
                av = ACC.v(i); sbv = SBB.v(i); mv = MIX.v(2 + i, ts)
                S.add(DVE, lambda e, av=av, sbv=sbv, mv=mv: e.tensor_tensor(out=mv.ap, in0=av.ap, in1=sbv.ap, op=ALU.mult),
                      reads=av.keys + sbv.keys, writes=mv.keys)
            srcv = UB.v(sl(0, 2), sl(SBW, SBW + 2))
            dstv = HU.v(li, sl(0, 2), sl(0, 2))
            S.add(ACT, lambda e, srcv=srcv, dstv=dstv: e.copy(out=dstv.ap, in_=srcv.ap), reads=srcv.keys, writes=dstv.keys)

            lo_, up_ = (0, 64), (64, 128)

            def padd(outv, av, bv):
                S.add(DVE, lambda e: e.tensor_tensor(out=outv.ap, in0=av.ap, in1=bv.ap, op=ALU.add),
                      reads=av.keys + bv.keys, writes=outv.keys)
            padd(PS2.v(0, sl(14, 528), p=up_), PB.v(0, sl(14, 528), p=up_), PB.v(0, sl(13, 527), p=up_))
            padd(PS2.v(1, sl(2, 528)), PB.v(1, sl(2, 528)), PB.v(1, sl(1, 527)))
            padd(SSEL.v(0, p=lo_), PB.v(0, sl(16, 528), p=lo_), PB.v(0, sl(15, 527), p=lo_))
            padd(PS4.v(sl(4, 528)), PS2.v(1, sl(4, 528)), PS2.v(1, sl(2, 526)))
            padd(SSEL.v(0, p=up_), PS2.v(0, sl(16, 528), p=up_), PS2.v(0, sl(14, 526), p=up_))
            padd(PS8.v(sl(8, 528), p=up_), PS4.v(sl(8, 528), p=up_), PS4.v(sl(4, 524), p=up_))
            padd(SSEL.v(1, p=lo_), PS4.v(sl(16, 528), p=lo_), PS4.v(sl(12, 524), p=lo_))
            padd(SSEL.v(1, p=up_), PS8.v(sl(16, 528), p=up_), PS8.v(sl(8, 520), p=up_))
            for i in range(2):
                ssv = SSEL.v(i); iw = cs("invw", i); pv = PB.v(i, sl(16, 16 + SBW)); dv = DD.v(i)
                S.add(DVE, lambda e, ssv=ssv, iw=iw, pv=pv, dv=dv: e.scalar_tensor_tensor(
                    out=dv.ap, in0=ssv.ap, scalar=iw.ap, in1=pv.ap, op0=ALU.mult, op1=ALU.subtract),
                    reads=ssv.keys + iw.keys + pv.keys, writes=dv.keys)
            if first_sb:
                for i in range(2):
                    ssv = SSEL.v(i, sl(0, 16)); ic = cs("invc", i * 16, 16); pt = PTMP.v(i)
                    S.add(DVE, lambda e, ssv=ssv, ic=ic, pt=pt: e.tensor_tensor(out=pt.ap, in0=ssv.ap, in1=ic.ap, op=ALU.mult),
                          reads=ssv.keys + ic.keys, writes=pt.keys)
                for i in range(2):
                    pt = PTMP.v(i); pv = PB.v(i, sl(16, 32)); dv = DD.v(i, sl(0, 16))
                    S.add(DVE, lambda e, pt=pt, pv=pv, dv=dv: e.tensor_tensor(out=dv.ap, in0=pt.ap, in1=pv.ap, op=ALU.subtract),
                          reads=pt.keys + pv.keys, writes=dv.keys)
            srcv = PB.v(sl(0, 2), sl(SBW, SBW + 16))
            dstv = HP.v(li, sl(0, 2), sl(0, 16))
            S.add(ACT, lambda e, srcv=srcv, dstv=dstv: e.copy(out=dstv.ap, in_=srcv.ap), reads=srcv.keys, writes=dstv.keys)

            def pvv(blk):
                return P(pvb[blk // 2], (blk % 2) * 256, (blk % 2) * 256 + 256)
            for blk in range(4):
                src = pvv(blk); bn = BNS.v(blk)
                S.add(DVE, lambda e, src=src, bn=bn: e.bn_stats(out=bn.ap, in_=src.ap), reads=src.keys, writes=bn.keys)
            for blk in range(4):
                bn = BNS.v(blk); mv_ = MV.v(blk)
                S.add(DVE, lambda e, bn=bn, mv_=mv_: e.bn_aggr(out=mv_.ap, in_=bn.ap), reads=bn.keys, writes=mv_.keys)
            var_ap = MV.t[:, :, 1]
            rsv = RS.v()
            S.add(ACT, lambda e: e.activation(out=rsv.ap, in_=var_ap, func=AF.Ln, bias=EPS, scale=1.0),
                  reads=MV.v().keys, writes=rsv.keys)
            S.add(ACT, lambda e: e.activation(out=rsv.ap, in_=rsv.ap, func=AF.Exp, scale=-0.5), reads=rsv.keys, writes=rsv.keys)
            for blk in range(4):
                src = pvv(blk); vt = VT.v(blk); mean = MV.v(blk, sl(0, 1)); r1 = RS.v(sl(blk, blk + 1))
                S.add(DVE, lambda e, src=src, vt=vt, mean=mean, r1=r1: e.tensor_scalar(
                    out=vt.ap, in0=src.ap, scalar1=mean.ap, scalar2=r1.ap, op0=ALU.subtract, op1=ALU.mult),
                    reads=src.keys + mean.keys + r1.keys, writes=vt.keys)
            glg = cs("glg", li * 256, 256)
            glb = cs("glb", li * 256, 256)
            for blk in range(4):
                vt = VT.v(blk)
                S.add(DVE, lambda e, vt=vt: e.tensor_tensor(out=vt.ap, in0=vt.ap, in1=glg.ap, op=ALU.mult),
                      reads=vt.keys + glg.keys, writes=vt.keys)
            o_glb = lo["glb"] + li * 256
            glb4 = CK_.t[:, o_glb:o_glb + 256].rearrange("p (i h c) -> p i h c", i=2, h=2)
            for hh in range(2):
                for blk in range(4):
                    vt4 = VT.t[:, blk, :].rearrange("p (i h c) -> p i h c", i=2, h=2)
                    outap = VNP.t[:, blk, :, hh, hh * 64:(hh + 1) * 64]
                    in0 = vt4[:, :, hh, :]
                    in1 = glb4[:, :, hh, :]
                    S.add(DVE, lambda e, outap=outap, in0=in0, in1=in1: e.tensor_tensor(out=outap, in0=in0, in1=in1, op=ALU.add),
                          reads=VT.v(blk).keys + glb.keys, writes=VNP.v(blk).keys)
            for i in range(2):
                pz = newps()
                mm_group(P(pz), [(DG.v(i, k), YB.v(i, sl(k, k + SBW))) for k in range(CK)])
                zv = Z.v(i); zq = ZSQ.v(i); bv = cs("cb", li * 2 + i)
                S.add(ACT, lambda e, zv=zv, pz=pz, bv=bv: e.activation(out=zv.ap, in_=P(pz).ap, func=AF.Identity, bias=bv.ap, scale=1.0),
                      reads=P(pz).keys + bv.keys, writes=zv.keys)
                S.add(ACT, lambda e, zq=zq, pz=pz, bv=bv: e.activation(out=zq.ap, in_=P(pz).ap, func=AF.Square, bias=bv.ap, scale=1.0),
                      reads=P(pz).keys + bv.keys, writes=zq.keys)
                zb = ZB.v(i)
                S.add(ACT, lambda e, zb=zb, pz=pz, bv=bv: e.activation(out=zb.ap, in_=P(pz).ap, func=AF.Identity, bias=bv.ap, scale=1.0),
                      reads=P(pz).keys + bv.keys, writes=zb.keys)
            for i in range(2):
                pq = newps()
                mm_group(P(pq), [(POOLW.v(li, i), DD.v(i))])
                mv = MIX.v(4 + i, ts); scl = cs("psc", li * 2 + i)
                S.add(DVE, lambda e, mv=mv, pq=pq, scl=scl: e.tensor_scalar(
                    out=mv.ap, in0=P(pq).ap, scalar1=scl.ap, scalar2=None, op0=ALU.mult),
                    reads=P(pq).keys + scl.keys, writes=mv.keys)

            pmxs = []
            for i in range(2):
                pmx = newps()
                for blk in range(4):
                    mm_group(P(pmx, blk * 128, blk * 128 + 128),
                             [(VNP.v(blk, i, hh), WST.v(li, 2 * i + hh)) for hh in range(2)])
                pmxs.append(pmx)
            pm = newps()
            mm_group(P(pm), [(ONES256.v(), ZB.v(i)) for i in range(2)])
            pe2 = newps()
            mm_group(P(pe2), [(ONES256.v(), ZSQ.v(i)) for i in range(2)])
            mean_v = MEZ.v(0); ez2_v = MEZ.v(1)
            S.add(ACT, lambda e, pm=pm, mean_v=mean_v: e.copy(out=mean_v.ap, in_=P(pm).ap), reads=P(pm).keys, writes=mean_v.keys)
            S.add(ACT, lambda e, pe2=pe2, ez2_v=ez2_v: e.copy(out=ez2_v.ap, in_=P(pe2).ap), reads=P(pe2).keys, writes=ez2_v.keys)

            for i in range(2):
                pmx = pmxs[i]
                t1 = T1.v(i)
                o_b = lo["bias"] + (li * 2 + i) * 128
                bias_bc = CK_.t[:, o_b:o_b + 128].unsqueeze(1).to_broadcast([128, 4, 128])
                t1_3 = T1.t[:, i, :].rearrange("p (a t) -> p a t", a=4)
                ps3 = ps_t[pmx][:, :].rearrange("p (a t) -> p a t", a=4)
                S.add(DVE, lambda e, t1_3=t1_3, ps3=ps3, bias_bc=bias_bc: e.tensor_tensor(out=t1_3, in0=ps3, in1=bias_bc, op=ALU.add),
                      reads=P(pmx).keys + cs("bias", (li * 2 + i) * 128, 128).keys, writes=t1.keys)
            for i in range(2):
                t1 = T1.v(i)
                guv = GU.v(i); mv = MIX.v(6 + i, ts)
                S.add(DVE, lambda e, t1=t1, guv=guv, mv=mv: e.tensor_tensor(out=mv.ap, in0=t1.ap, in1=guv.ap, op=ALU.mult),
                      reads=t1.keys + guv.keys, writes=mv.keys)

            l0 = LNA.v(0); l1 = LNA.v(1); l2 = LNA.v(2)
            S.add(ACT, lambda e: e.activation(out=l0.ap, in_=mean_v.ap, func=AF.Square), reads=mean_v.keys, writes=l0.keys)
            S.add(DVE, lambda e: e.tensor_tensor(out=l1.ap, in0=ez2_v.ap, in1=l0.ap, op=ALU.subtract),
                  reads=ez2_v.keys + l0.keys, writes=l1.keys)
            S.add(ACT, lambda e: e.activation(out=l1.ap, in_=l1.ap, func=AF.Ln, bias=EPS, scale=1.0), reads=l1.keys, writes=l1.keys)
            S.add(ACT, lambda e: e.activation(out=l1.ap, in_=l1.ap, func=AF.Exp, scale=-0.5), reads=l1.keys, writes=l1.keys)
            S.add(DVE, lambda e: e.tensor_tensor(out=l2.ap, in0=mean_v.ap, in1=l1.ap, op=ALU.mult),
                  reads=mean_v.keys + l1.keys, writes=l2.keys)
            for i in range(2):
                zv = Z.v(i)
                S.add(DVE, lambda e, zv=zv: e.tensor_tensor(out=zv.ap, in0=zv.ap, in1=l1.ap, op=ALU.mult),
                      reads=zv.keys + l1.keys, writes=zv.keys)
            for i in range(2):
                zv = Z.v(i)
                S.add(DVE, lambda e, zv=zv: e.tensor_tensor(out=zv.ap, in0=zv.ap, in1=l2.ap, op=ALU.subtract),
                      reads=zv.keys + l2.keys, writes=zv.keys)
            for i in range(2):
                zv = Z.v(i)
                mv = MIX.v(i, ts)
                gv = cs("clg", li * 2 + i); bv = cs("clb", li * 2 + i)
                S.add(ACT, lambda e, zv=zv, mv=mv, gv=gv, bv=bv: e.activation(
                    out=mv.ap, in_=zv.ap, func=AF.Silu, bias=bv.ap, scale=gv.ap),
                    reads=zv.keys + gv.keys + bv.keys, writes=mv.keys)

        pinned.discard(sD_)
        def oitem(c, sb):
            ts = sl(sb * SBW, (sb + 1) * SBW)
            po = newps()
            mm_group(P(po), [(RIN[sO].v(m, sl(c * 128, c * 128 + 128)), MIX.v(m, ts)) for m in range(8)])
            xv = X.v(c, ts)
            S.add(DVE, lambda e, xv=xv, po=po: e.tensor_tensor(out=xv.ap, in0=P(po).ap, in1=xv.ap, op=ALU.add),
                  reads=P(po).keys + xv.keys, writes=xv.keys)
        run_tail([lambda c=c: oitem(c, 0) for c in range(8)], [lambda c=c: oitem(c, 1) for c in range(8)], nxt)
        if nxt is not None:
            nxt.pb["pre"] = True

    def final_norm_inplace(sb):
        if True:
            ts = sl(sb * SBW, (sb + 1) * SBW)
            xin = X.v(sl(0, 8), ts)
            sq = SQ.v()
            S.add(ACT, lambda e, xin=xin, sq=sq: e.activation(out=sq.ap, in_=xin.ap, func=AF.Square),
                  reads=xin.keys, writes=sq.keys)
            pb = newps()
            mm_group(P(pb), [(ONESD.v(), SQ.v(c)) for c in range(8)])
            rs = RSTD.v(sb)
            S.add(ACT, lambda e, rs=rs, pb=pb: e.activation(out=rs.ap, in_=P(pb).ap, func=AF.Ln, bias=EPS, scale=1.0),
                  reads=P(pb).keys, writes=rs.keys)
            S.add(ACT, lambda e, rs=rs: e.activation(out=rs.ap, in_=rs.ap, func=AF.Exp, scale=-0.5), reads=rs.keys, writes=rs.keys)
            for c in range(8):
                xv = X.v(c, ts)
                gv = cs("nf", c)
                S.add(DVE, lambda e, xv=xv, gv=gv, rs=rs: e.scalar_tensor_tensor(
                    out=xv.ap, in0=xv.ap, scalar=gv.ap, in1=rs.ap, op0=ALU.mult, op1=ALU.mult),
                    reads=xv.keys + gv.keys + rs.keys, writes=xv.keys)

    xTv = xT.rearrange("(c p) s -> p c s", p=128)
    yTv = yT.rearrange("(c p) s -> p c s", p=128)
    for it in range(NT):
        t0 = it * T
        for sb in range(NSB):
            xh = X.v(sl(0, 8), sl(sb * SBW, (sb + 1) * SBW))
            a = t0 + sb * SBW
            S.add(SP, lambda e, xh=xh, a=a: e.dma_start(out=xh.ap, in_=xTv[:, :, a:a + SBW]), writes=xh.keys, dma_sem=f"xl{sb}")
        full = all(k in DBG_STAGES for k in ("ffn1", "mixer", "ffn2"))
        n1 = Norm("n1", 0)
        for li in range(nL):
            nm = Norm("nm", li)
            n2 = Norm("n2", li)
            n1_next = Norm("n1", li + 1) if li + 1 < nL else None
            if "ffn1" in DBG_STAGES:
                ffn("ffn1", n1, li, mid_hook=(lambda li=li: build_diag(li)) if "mixer" in DBG_STAGES else None,
                    nxt=nm if full else None)
            elif "mixer" in DBG_STAGES:
                build_diag(li)
            if "mixer" in DBG_STAGES:
                mixer(li, it == 0, nm, nxt=n2 if full else None)
            if "ffn2" in DBG_STAGES:
                ffn("ffn2", n2, li, nxt=n1_next if full else None)
            n1 = n1_next
        for sb in range(NSB):
            if with_final:
                final_norm_inplace(sb)
            xh = X.v(sl(0, 8), sl(sb * SBW, (sb + 1) * SBW))
            a = t0 + sb * SBW
            S.add(SP, lambda e, xh=xh, a=a: e.dma_start(out=yTv[:, :, a:a + SBW], in_=xh.ap), reads=xh.keys,
                  writes=(("OUT", sb),), dma_sem=f"xs{sb}")
    S.add(SP, lambda e: e.nop(), reads=tuple(("OUT", sb) for sb in range(NSB)))

    with contextlib.ExitStack() as es:
        sems = {e_: es.enter_context(nc.semaphore("s_" + e_)) for e_ in ENGS}
        dsems = {k: es.enter_context(nc.semaphore("d_" + k)) for k in S.dma_cnt}
        block = es.enter_context(nc.Block())
        S.emit(block, sems, dsems)
    return nc


_W_NAMES = ("ffn1_w1", "ffn1_w3", "ffn1_w2", "w_in", "w_out", "ffn2_w1", "ffn2_w3", "ffn2_w2")
_PROG_CACHE = {}


def _get_prog(S_len, nL, with_final):
    key = (S_len, nL, with_final)
    if key not in _PROG_CACHE:
        _PROG_CACHE[key] = build_program(S_len, nL, with_final)
    return _PROG_CACHE[key]


def run_layers(xT_list, inp, layers, with_final):
    S_len = xT_list[0].shape[1]
    nL = len(layers)
    nc = _get_prog(S_len, nL, with_final)
    cst, _ = _pack_consts(inp, layers, with_final)
    ws = {n: np.ascontiguousarray(np.asarray(inp[n], np.float32)[list(layers)]) for n in _W_NAMES}
    in_maps = []
    for x in xT_list:
        m = {"xT": np.ascontiguousarray(x, dtype=np.float32), "cst": cst}
        m.update(ws)
        in_maps.append(m)
    res = run_bass_kernel_spmd(nc, in_maps, core_ids=list(range(len(xT_list))))
    return [r["yT"] for r in res.results]


FUSED = True


def kernel(**inputs):
    x = np.asarray(inputs["x"], np.float32)
    B = x.shape[0]
    xT_list = [np.ascontiguousarray(x[b].T) for b in range(B)]
    if FUSED:
        outs = run_layers(xT_list, inputs, list(range(DEPTH)), True)
    else:
        outs = xT_list
        for l in range(DEPTH):
            outs = run_layers(outs, inputs, [l], l == DEPTH - 1)
    return np.stack([np.ascontiguousarray(o.T) for o in outs], axis=0).astype(np.float32)
```
